# Optimizing a Trainium2 kernel written in Bass

```python
import jax, jax.numpy as jnp
from jax import lax
import numpy as np


D_MODEL = 2048
BATCH = 4
SEQ = 4096
DEPTH = 4

CHUNK = 64
N_MIXERS = 2
N_POOL_LAYERS = (DEPTH + N_MIXERS - 1) // N_MIXERS
N_LRU_LAYERS = DEPTH // N_MIXERS
POOL_WINDOWS = (2, 4, 8, 16)
POOL_GROUPS = 4
POOL_GROUP_DIM = D_MODEL // POOL_GROUPS
LRU_WIDTH = D_MODEL
LRU_HEADS = 16
LRU_HEAD_DIM = LRU_WIDTH // LRU_HEADS
CONV_WIDTH = 4
LRU_C = 8.0
FFN_DIM = 256 * ((8 * D_MODEL // 3 + 255) // 256)
MEM_LEN = 256
XATTN_HEADS = 4
XATTN_HEAD_DIM = D_MODEL // XATTN_HEADS
MACARON_WEIGHT = 0.5
EPS = 1e-6

kernel_name = 'hybrid_pool_rglru_macaron_memxattn'


def rmsnorm(x, g):
    xf = x.astype(jnp.float32)
    y = xf * lax.rsqrt(jnp.mean(xf * xf, axis=-1, keepdims=True) + EPS)
    return (y * g.astype(jnp.float32)).astype(x.dtype)


def swiglu(u, w_gate, w_up, w_down):
    return (jax.nn.silu(u @ w_gate) * (u @ w_up)) @ w_down


def pool_mixer(u, w_group, scale):
    b, s, d = u.shape
    uf = u.astype(jnp.float32).reshape(b, s, POOL_GROUPS, POOL_GROUP_DIM)
    cs = jnp.concatenate([jnp.zeros((b, 1, POOL_GROUPS, POOL_GROUP_DIM), jnp.float32),
                          jnp.cumsum(uf, axis=1)], axis=1)
    pos = jnp.arange(1, s + 1, dtype=jnp.float32)[None, :, None]
    outs = []
    for g, w in enumerate(POOL_WINDOWS):
        c = cs[:, :, g]
        lower = jnp.concatenate([jnp.zeros((b, w - 1, POOL_GROUP_DIM), jnp.float32),
                                 c[:, :s + 1 - w]], axis=1)
        count = jnp.minimum(pos, float(w))
        outs.append((c[:, 1:] - lower) / count - uf[:, :, g])
    pooled = jnp.stack(outs, axis=2).astype(u.dtype)
    y = jnp.einsum('bsgi,gij->bsgj', pooled, w_group).reshape(b, s, d)
    return y * scale


def rglru_block(u, w_in, conv_w, conv_b, w_a, b_a, w_x, b_x, lam, w_out):
    b, s, _ = u.shape
    proj = u @ w_in
    gate, xr = proj[..., :LRU_WIDTH], proj[..., LRU_WIDTH:]
    xp = jnp.pad(xr, ((0, 0), (CONV_WIDTH - 1, 0), (0, 0)))
    xc = conv_b
    for k in range(CONV_WIDTH):
        xc = xc + xp[:, k:k + s] * conv_w[k]
    xh = xc.reshape(b, s, LRU_HEADS, LRU_HEAD_DIM)
    r = jax.nn.sigmoid((jnp.einsum('bshi,hij->bshj', xh, w_a).reshape(b, s, LRU_WIDTH) + b_a).astype(jnp.float32))
    ig = jax.nn.sigmoid((jnp.einsum('bshi,hij->bshj', xh, w_x).reshape(b, s, LRU_WIDTH) + b_x).astype(jnp.float32))
    log_a = -LRU_C * r * jax.nn.softplus(-lam.astype(jnp.float32))
    a = jnp.exp(log_a)
    bterm = jnp.sqrt(-jnp.expm1(2.0 * log_a)) * ig * xc.astype(jnp.float32)

    def combine(lhs, rhs):
        a1, b1 = lhs
        a2, b2 = rhs
        return a1 * a2, a2 * b1 + b2

    _, h = lax.associative_scan(combine, (a, bterm), axis=1)
    y = h.astype(u.dtype) * jax.nn.gelu(gate)
    return y @ w_out


def mem_cross_attention(u, m, w_q, w_k, w_v, w_o):
    b, s, d = u.shape
    ml = m.shape[1]
    q = (u @ w_q).reshape(b, s, XATTN_HEADS, XATTN_HEAD_DIM)
    k = (m @ w_k).reshape(b, ml, XATTN_HEADS, XATTN_HEAD_DIM)
    v = (m @ w_v).reshape(b, ml, XATTN_HEADS, XATTN_HEAD_DIM)
    scores = jnp.einsum('bshd,bmhd->bhsm', q, k).astype(jnp.float32) * (XATTN_HEAD_DIM ** -0.5)
    p = jax.nn.softmax(scores, axis=-1).astype(v.dtype)
    o = jnp.einsum('bhsm,bmhd->bshd', p, v).reshape(b, s, d)
    return o @ w_o


def setup_inputs(seed: int = 0) -> dict:
    key = jax.random.key(seed)
    ks = jax.random.split(key, 26)
    f32 = jnp.float32

    def nrm(k, shape, fan_in):
        return jax.random.normal(k, shape, f32) * (fan_in ** -0.5)

    def gain(k, shape):
        return 1.0 + 0.02 * jax.random.normal(k, shape, f32)

    a8 = jax.random.uniform(ks[16], (N_LRU_LAYERS, LRU_WIDTH), f32, 0.9, 0.999)
    s_lam = a8 ** (1.0 / LRU_C)
    lru_lambda = jnp.log(s_lam) - jnp.log1p(-s_lam)
    return {
        'x': jax.random.normal(ks[0], (BATCH, SEQ, D_MODEL), f32),
        'mem': jax.random.normal(ks[1], (BATCH, MEM_LEN, D_MODEL), f32),
        'ffn_norm': gain(ks[2], (DEPTH, 2, D_MODEL)),
        'w_ffn_gate': nrm(ks[3], (DEPTH, 2, D_MODEL, FFN_DIM), D_MODEL),
        'w_ffn_up': nrm(ks[4], (DEPTH, 2, D_MODEL, FFN_DIM), D_MODEL),
        'w_ffn_down': nrm(ks[5], (DEPTH, 2, FFN_DIM, D_MODEL), FFN_DIM),
        'mix_norm': gain(ks[6], (DEPTH, D_MODEL)),
        'pool_w': nrm(ks[7], (N_POOL_LAYERS, POOL_GROUPS, POOL_GROUP_DIM, POOL_GROUP_DIM), POOL_GROUP_DIM),
        'pool_scale': gain(ks[8], (N_POOL_LAYERS, D_MODEL)),
        'lru_w_in': nrm(ks[9], (N_LRU_LAYERS, D_MODEL, 2 * LRU_WIDTH), D_MODEL),
        'lru_conv_w': nrm(ks[10], (N_LRU_LAYERS, CONV_WIDTH, LRU_WIDTH), CONV_WIDTH),
        'lru_conv_b': 0.01 * jax.random.normal(ks[11], (N_LRU_LAYERS, LRU_WIDTH), f32),
        'lru_w_a': nrm(ks[12], (N_LRU_LAYERS, LRU_HEADS, LRU_HEAD_DIM, LRU_HEAD_DIM), LRU_HEAD_DIM),
        'lru_b_a': 0.01 * jax.random.normal(ks[13], (N_LRU_LAYERS, LRU_WIDTH), f32),
        'lru_w_x': nrm(ks[14], (N_LRU_LAYERS, LRU_HEADS, LRU_HEAD_DIM, LRU_HEAD_DIM), LRU_HEAD_DIM),
        'lru_b_x': 0.01 * jax.random.normal(ks[15], (N_LRU_LAYERS, LRU_WIDTH), f32),
        'lru_lambda': lru_lambda,
        'lru_w_out': nrm(ks[17], (N_LRU_LAYERS, LRU_WIDTH, D_MODEL), LRU_WIDTH),
        'xattn_norm': gain(ks[18], (DEPTH, D_MODEL)),
        'mem_norm': gain(ks[19], (D_MODEL,)),
        'w_q': nrm(ks[20], (DEPTH, D_MODEL, D_MODEL), D_MODEL),
        'w_k': nrm(ks[21], (DEPTH, D_MODEL, D_MODEL), D_MODEL),
        'w_v': nrm(ks[22], (DEPTH, D_MODEL, D_MODEL), D_MODEL),
        'w_o': nrm(ks[23], (DEPTH, D_MODEL, D_MODEL), D_MODEL),
        'final_norm': gain(ks[24], (D_MODEL,)),
    }


def reference(x, mem, ffn_norm, w_ffn_gate, w_ffn_up, w_ffn_down, mix_norm, pool_w, pool_scale,
              lru_w_in, lru_conv_w, lru_conv_b, lru_w_a, lru_b_a, lru_w_x, lru_b_x, lru_lambda,
              lru_w_out, xattn_norm, mem_norm, w_q, w_k, w_v, w_o, final_norm):
    m = rmsnorm(mem, mem_norm)
    h = x
    for i in range(DEPTH):
        h = h + MACARON_WEIGHT * swiglu(rmsnorm(h, ffn_norm[i, 0]), w_ffn_gate[i, 0], w_ffn_up[i, 0], w_ffn_down[i, 0])
        u = rmsnorm(h, mix_norm[i])
        j = i // N_MIXERS
        if i % N_MIXERS == 0:
            h = h + pool_mixer(u, pool_w[j], pool_scale[j])
        else:
            h = h + rglru_block(u, lru_w_in[j], lru_conv_w[j], lru_conv_b[j], lru_w_a[j], lru_b_a[j],
                                lru_w_x[j], lru_b_x[j], lru_lambda[j], lru_w_out[j])
        h = h + mem_cross_attention(rmsnorm(h, xattn_norm[i]), m, w_q[i], w_k[i], w_v[i], w_o[i])
        h = h + MACARON_WEIGHT * swiglu(rmsnorm(h, ffn_norm[i, 1]), w_ffn_gate[i, 1], w_ffn_up[i, 1], w_ffn_down[i, 1])
    return rmsnorm(h, final_norm)
```

```python
import contextlib
import numpy as np
import concourse.bass as bass
import concourse.mybir as mybir
from concourse.bass_utils import run_bass_kernel_spmd

F32 = mybir.dt.float32
BF16 = mybir.dt.bfloat16
ALU = mybir.AluOpType
AF = mybir.ActivationFunctionType

D = 2048
DC = 16
FF = 5632
FC = 44
NTOK = 2048
T = 512
NT = NTOK // T
ML = 256
NH = 4
EPS = 1e-6
HALO = 16
NCORES = 8
XDBG = 0


class Buf:
    __slots__ = ("name", "w", "rs")

    def __init__(self, name):
        self.name = name
        self.w = None
        self.rs = {}


class Prog:
    ENGS = ("pe", "act", "dve", "pool", "sp")

    def __init__(self, nc, ndma=32):
        self.nc = nc
        self.es = contextlib.ExitStack()
        self.q = {e: [] for e in self.ENGS}
        self.cnt = {e: 0 for e in self.ENGS}
        self.waited = {e: {} for e in self.ENGS}
        self.sems = {}
        for e in self.ENGS:
            self.sems[e] = self.es.enter_context(nc.semaphore("s_" + e))
        self.ndma = ndma
        for i in range(ndma):
            self.sems[("d", i)] = self.es.enter_context(nc.semaphore("s_d%d" % i))
        self.dma_k = 0
        self.last_dma = {}
        self.nuid = 0

    def sbuf(self, name, shape, dtype, stack=None):
        self.nuid += 1
        return (stack or self.es).enter_context(
            self.nc.sbuf_tensor("%s_%d" % (name, self.nuid), list(shape), dtype))

    def psum(self, name, shape, dtype=F32):
        return self.es.enter_context(self.nc.psum_tensor(name, list(shape), dtype))

    def _deps(self, eng, reads, writes):
        deps = {}

        def add(ev):
            if ev is None:
                return
            s, v = ev
            if s == eng and eng == "pe":
                return
            if deps.get(s, 0) < v:
                deps[s] = v

        for b in reads:
            add(b.w)
        for b in writes:
            add(b.w)
            for s, v in b.rs.items():
                add((s, v))
        w = self.waited[eng]
        out = []
        for s, v in deps.items():
            if w.get(s, 0) < v:
                w[s] = v
                out.append((s, v))
        return out

    def _mark(self, ev, reads, writes):
        s, v = ev
        for b in reads:
            if b.rs.get(s, 0) < v:
                b.rs[s] = v
        for b in writes:
            b.w = ev
            b.rs = {}

    def op(self, eng, fn, reads=(), writes=()):
        waits = self._deps(eng, reads, writes)
        self.cnt[eng] += 1
        ev = (eng, self.cnt[eng])
        self._mark(ev, reads, writes)
        sems = self.sems
        sem_own = sems[eng]

        def run(e):
            for s, v in waits:
                e.wait_ge(sems[s], v)
            fn(e).then_inc(sem_own, 1)

        self.q[eng].append(run)
        return ev

    def dma(self, qeng, out_ap, in_ap, reads=(), writes=(), **kw):
        k = self.dma_k
        self.dma_k += 1
        i, n = k % self.ndma, k // self.ndma
        skey = ("d", i)
        waits = self._deps(qeng, reads, writes)
        w = self.waited[qeng]
        if n > 0 and w.get(skey, 0) < 16 * n:
            w[skey] = 16 * n
            waits.append((skey, 16 * n))
        ev = (skey, 16 * (n + 1))
        self.last_dma[skey] = ev
        self._mark(ev, reads, writes)
        sems = self.sems

        def run(e):
            for s, v in waits:
                e.wait_ge(sems[s], v)
            e.dma_start(out=out_ap, in_=in_ap, **kw).then_inc(sems[skey], 16)

        self.q[qeng].append(run)
        return ev

    def wait_event(self, eng, ev):
        s, v = ev
        w = self.waited[eng]
        if w.get(s, 0) < v:
            w[s] = v
            sems = self.sems
            self.q[eng].append(lambda e: e.wait_ge(sems[s], v))

    def wait_bufs(self, eng, bufs):
        for b in bufs:
            if b.w is not None:
                self.wait_event(eng, b.w)

    def barrier(self):
        evs = [(e, self.cnt[e]) for e in self.ENGS if self.cnt[e] > 0] + list(self.last_dma.values())
        for e in self.ENGS:
            for ev in evs:
                if ev[0] != e:
                    self.wait_event(e, ev)

    def emit(self):
        nc = self.nc
        q = self.q
        with nc.Block() as block:
            @block.tensor
            def _(e):
                for f in q["pe"]:
                    f(e)

            @block.scalar
            def _(e):
                for f in q["act"]:
                    f(e)

            @block.vector
            def _(e):
                for f in q["dve"]:
                    f(e)

            @block.gpsimd
            def _(e):
                for f in q["pool"]:
                    f(e)

            @block.sync
            def _(e):
                for f in q["sp"]:
                    f(e)

    def close(self):
        self.es.close()


class Ctx:
    def __init__(self, nc):
        self.nc = nc
        P = self.P = Prog(nc)
        self.banks = [P.psum("bank%d" % i, [128, 512]) for i in range(5)]
        self.bankT = P.psum("bankT", [128, 1024], BF16)
        self.banks.append(None)
        self.banks.append(P.psum("bank6", [128, 512]))
        self.banks.append(P.psum("bank7", [128, 512]))
        self.onec = P.sbuf("onec", [128, 1], F32)
        self.b_onec = Buf("onec")
        P.op("pool", lambda e: e.memset(self.onec[:], 1.0), writes=[self.b_onec])
        self.bbuf = [Buf("bank%d" % i) for i in range(8)]
        self.mm_rr = 0
        self.ones = P.sbuf("ones", [128, 128], F32)
        self.b_ones = Buf("ones")
        P.op("pool", lambda e: e.memset(self.ones[:], 1.0), writes=[self.b_ones])
        self.epsc = P.sbuf("epsc", [128, 1], F32)
        self.b_eps = Buf("epsc")
        P.op("pool", lambda e: e.memset(self.epsc[:], EPS), writes=[self.b_eps])
        self.evac_rr = 0
        self.cache = {}

    def scratch(self, st, name, shape, dtype, n=1):
        key = (name, tuple(shape), str(dtype), n)
        if key not in self.cache:
            self.cache[key] = [(self.P.sbuf(name, shape, dtype, st), Buf(name + str(i))) for i in range(n)]
        return self.cache[key]

    def end_phase(self):
        self.P.barrier()
        self.cache = {}

    def mm_bank(self):
        i = self.mm_rr % 5
        self.mm_rr += 1
        return self.banks[i], self.bbuf[i]


def hview(h_dram):
    return h_dram.rearrange("(c p) t -> p c t", p=128)


def rmsnorm_tile(cx, st, h, bh, gain, bg, n, out, bout, out_dt_is_f32=False):
    P = cx.P
    sqs = cx.scratch(st, "sq", [128, n], F32, 2)
    sq = [x[0] for x in sqs]
    bsq = [x[1] for x in sqs]
    rstd, brstd = cx.scratch(st, "rstd", [128, n], F32, 1)[0]
    ms, bms = cx.banks[7], cx.bbuf[7]
    for c in range(DC):
        j = c % 2
        P.op("act", lambda e, c=c, j=j: e.activation(sq[j][:], h[:, c, :], AF.Square),
             reads=[bh], writes=[bsq[j]])
        P.op("pe", lambda e, c=c, j=j: e.matmul(ms[:, 0:n], cx.ones[:], sq[j][:], start=(c == 0), stop=(c == DC - 1)),
             reads=[bsq[j], cx.b_ones], writes=[bms])
    P.op("act", lambda e: e.activation(rstd[:], ms[:, 0:n], AF.Sqrt, bias=cx.epsc[:], scale=1.0 / D),
         reads=[bms, cx.b_eps], writes=[brstd])
    P.op("dve", lambda e: e.reciprocal(rstd[:], rstd[:]), reads=[brstd], writes=[brstd])
    for c in range(DC):
        P.op("dve", lambda e, c=c: e.scalar_tensor_tensor(out[:, c, :], h[:, c, :], gain[:, c:c + 1], rstd[:],
                                                          ALU.mult, ALU.mult),
             reads=[bh, bg, brstd], writes=[bout])


def stream_proj(cx, st, jobs, KC, rhs_fn, rhs_bufs, n, evac, nslab=3, tag="w"):
    P = cx.P
    sl = cx.scratch(st, "slab_" + tag, [128, KC, 128], BF16, nslab)
    slabs = [x[0] for x in sl]
    bsl = [x[1] for x in sl]
    for ji, (wap, key) in enumerate(jobs):
        s = ji % nslab
        P.dma("pool", slabs[s][:], wap.rearrange("(k p) n -> p k n", p=128), writes=[bsl[s]])
        bank, bb = cx.mm_bank()
        for kc in range(KC):
            P.op("pe", lambda e, kc=kc, s=s, bank=bank: e.matmul(bank[:, 0:n], slabs[s][:, kc, :], rhs_fn(kc),
                                                                 start=(kc == 0), stop=(kc == KC - 1)),
                 reads=[bsl[s]] + list(rhs_bufs), writes=[bb])
        evac(key, bank[:, 0:n], bb)


def load_vec(cx, st, dram_ap, name, ncol=DC):
    P = cx.P
    t = P.sbuf(name, [128, ncol], F32, st)
    b = Buf(name)
    P.dma("sp", t[:], dram_ap, writes=[b])
    return t, b


def phase_ffn(cx, hin, hout, gain_d, wg, wu, wd):
    P = cx.P
    with contextlib.ExitStack() as st:
        gain, bg = load_vec(cx, st, gain_d, "gain")
        hb = [P.sbuf("hb", [128, DC, T], F32, st) for _ in range(2)]
        bhb = [Buf("hb0"), Buf("hb1")]
        u = P.sbuf("u", [128, DC, T], BF16, st)
        bu = Buf("u")
        act = P.sbuf("act", [128, FC, T], BF16, st)
        bact = Buf("act")
        sg = [P.sbuf("sg", [128, T], F32, st) for _ in range(2)]
        bsg = [Buf("sg0"), Buf("sg1")]
        bhin, bhout = Buf("hin"), Buf("hout")
        hin_v, hout_v = hview(hin), hview(hout)
        P.dma("sp", hb[0][:], hin_v[:, :, 0:T], reads=[bhin], writes=[bhb[0]])
        for t in range(NT):
            h, bh = hb[t % 2], bhb[t % 2]
            if t + 1 < NT:
                P.dma("sp", hb[(t + 1) % 2][:], hin_v[:, :, (t + 1) * T:(t + 2) * T], reads=[bhin],
                      writes=[bhb[(t + 1) % 2]])
            rmsnorm_tile(cx, st, h, bh, gain, bg, T, u, bu)
            state = {}

            def evac_gu(key, ps, bb):
                kind, fc = key
                if kind == "g":
                    j = fc % 2
                    P.op("act", lambda e, j=j, ps=ps: e.activation(sg[j][:], ps, AF.Silu),
                         reads=[bb], writes=[bsg[j]])
                else:
                    j = fc % 2
                    P.op("dve", lambda e, j=j, ps=ps, fc=fc: e.tensor_tensor(act[:, fc, :], sg[j][:], ps, ALU.mult),
                         reads=[bb, bsg[j]], writes=[bact])

            jobs = []
            for fc in range(FC):
                jobs.append((wg[:, fc * 128:(fc + 1) * 128], ("g", fc)))
                jobs.append((wu[:, fc * 128:(fc + 1) * 128], ("u", fc)))
            stream_proj(cx, st, jobs, DC, lambda kc: u[:, kc, :], [bu], T, evac_gu, nslab=4, tag="gu")

            def evac_d(dc, ps, bb, h=h, bh=bh):
                P.op("dve", lambda e, dc=dc, ps=ps: e.scalar_tensor_tensor(h[:, dc, :], ps, 0.5, h[:, dc, :],
                                                                           ALU.mult, ALU.add),
                     reads=[bb, bh], writes=[bh])

            jobs = [(wd[:, dc * 128:(dc + 1) * 128], dc) for dc in range(DC)]
            stream_proj(cx, st, jobs, FC, lambda kc: act[:, kc, :], [bact], T, evac_d, nslab=2, tag="d")
            P.dma("sp", hout_v[:, :, t * T:(t + 1) * T], h[:], reads=[bh], writes=[bhout])
        cx.end_phase()
    return


def phase_final(cx, hin, out_d, gain_d):
    P = cx.P
    with contextlib.ExitStack() as st:
        gain, bg = load_vec(cx, st, gain_d, "gain")
        hb = [P.sbuf("hb", [128, DC, T], F32, st) for _ in range(2)]
        bhb = [Buf("hb0"), Buf("hb1")]
        ob = [P.sbuf("ob", [128, DC, T], F32, st) for _ in range(2)]
        bob = [Buf("ob0"), Buf("ob1")]
        bhin, bout = Buf("hin"), Buf("out")
        hin_v, out_v = hview(hin), hview(out_d)
        for t in range(NT):
            h, bh = hb[t % 2], bhb[t % 2]
            P.dma("sp", h[:], hin_v[:, :, t * T:(t + 1) * T], reads=[bhin], writes=[bh])
            rmsnorm_tile(cx, st, h, bh, gain, bg, T, ob[t % 2], bob[t % 2])
            P.dma("sp", out_v[:, :, t * T:(t + 1) * T], ob[t % 2][:], reads=[bob[t % 2]], writes=[bout])
        cx.end_phase()


def phase_pool(cx, hin, hout, halo_d, gain_d, pscale_d, invcnt_d, pw):
    P = cx.P
    W = HALO + T
    with contextlib.ExitStack() as st:
        gain, bg = load_vec(cx, st, gain_d, "gain")
        pscale, bps = load_vec(cx, st, pscale_d, "pscale")
        invc, binv = load_vec(cx, st, invcnt_d, "invc", 64)
        hb = [P.sbuf("hb", [128, DC, T], F32, st) for _ in range(2)]
        bhb = [Buf("hb0"), Buf("hb1")]
        hh = P.sbuf("hh", [128, DC, HALO], F32, st)
        bhh = Buf("hh")
        uf = [P.sbuf("uf", [128, DC, W], F32, st) for _ in range(2)]
        buf_ = [Buf("uf0"), Buf("uf1")]
        lv = [P.sbuf("lv", [128, 4, W], F32, st) for _ in range(2)]
        blv = [Buf("lv0"), Buf("lv1")]
        pooled = P.sbuf("pooled", [128, DC, T], BF16, st)
        bpo = Buf("pooled")
        tmp16 = P.sbuf("tmp16", [128, 4, HALO], F32, st)
        bt16 = Buf("tmp16")
        bhin, bhout, bhalo = Buf("hin"), Buf("hout"), Buf("halo")
        hin_v, hout_v = hview(hin), hview(hout)
        P.dma("sp", hh[:], hview(halo_d), reads=[bhalo], writes=[bhh])
        ufh = P.sbuf("ufh", [128, DC, HALO], F32, st)
        bufh = Buf("ufh")
        rmsnorm_tile(cx, st, hh, bhh, gain, bg, HALO, ufh, bufh)
        P.op("pool", lambda e: e.tensor_copy(uf[0][:, :, 0:HALO], ufh[:]), reads=[bufh], writes=[buf_[0]])
        for t in range(NT):
            h, bh = hb[t % 2], bhb[t % 2]
            U, bU = uf[t % 2], buf_[t % 2]
            P.dma("sp", h[:], hin_v[:, :, t * T:(t + 1) * T], reads=[bhin], writes=[bh])
            rmsnorm_tile(cx, st, h, bh, gain, bg, T, U[:, :, HALO:W], bU)
            if t + 1 < NT:
                P.op("pool", lambda e, U=U, t=t: e.tensor_copy(uf[(t + 1) % 2][:, :, 0:HALO], U[:, :, T:W]),
                     reads=[bU], writes=[buf_[(t + 1) % 2]])
            for g in range(4):
                cs = slice(4 * g, 4 * g + 4)
                src, bsrc = U, bU
                sh = 1
                for lvl in range(g + 1):
                    dst, bdst = lv[lvl % 2], blv[lvl % 2]
                    lo = 2 * sh - 1
                    if lvl == 0:
                        P.op("dve", lambda e, dst=dst, lo=lo, sh=sh, cs=cs, U=U: e.tensor_tensor(
                            dst[:, :, lo:W], U[:, cs, lo:W], U[:, cs, lo - sh:W - sh], ALU.add),
                            reads=[bU], writes=[bdst])
                    else:
                        P.op("dve", lambda e, dst=dst, src=src, lo=lo, sh=sh: e.tensor_tensor(
                            dst[:, :, lo:W], src[:, :, lo:W], src[:, :, lo - sh:W - sh], ALU.add),
                            reads=[bsrc], writes=[bdst])
                    src, bsrc = dst, bdst
                    sh *= 2
                w = 2 ** (g + 1)
                for k in range(4):
                    c = 4 * g + k
                    P.op("dve", lambda e, src=src, k=k, c=c, w=w, U=U: e.scalar_tensor_tensor(
                        pooled[:, c, :], src[:, k, HALO:W], 1.0 / w, U[:, c, HALO:W], ALU.mult, ALU.subtract),
                        reads=[bsrc, bU], writes=[bpo])
                if t == 0:
                    P.op("dve", lambda e, src=src, g=g: e.tensor_tensor(
                        tmp16[:], src[:, :, HALO:2 * HALO], invc[:, g * 16:(g + 1) * 16].unsqueeze(1).to_broadcast([128, 4, HALO]),
                        ALU.mult), reads=[bsrc, binv], writes=[bt16])
                    P.op("dve", lambda e, cs=cs, U=U: e.tensor_tensor(
                        pooled[:, cs, 0:HALO], tmp16[:], U[:, cs, HALO:2 * HALO], ALU.subtract),
                        reads=[bt16, bU], writes=[bpo])

                def evac(jc, ps, bb, g=g, h=h, bh=bh):
                    c = 4 * g + jc
                    P.op("dve", lambda e, c=c, ps=ps: e.scalar_tensor_tensor(
                        h[:, c, :], ps, pscale[:, c:c + 1], h[:, c, :], ALU.mult, ALU.add),
                        reads=[bb, bps, bh], writes=[bh])

                jobs = [(pw[g, :, jc * 128:(jc + 1) * 128], jc) for jc in range(4)]
                stream_proj(cx, st, jobs, 4, lambda kc, g=g: pooled[:, 4 * g + kc, :], [bpo], T, evac, nslab=2,
                            tag="pw")
            P.dma("sp", hout_v[:, :, t * T:(t + 1) * T], h[:], reads=[bh], writes=[bhout])
        cx.end_phase()


NLV = 9


def phase_lru(cx, hin, hout, halo_d, carry_in_d, carry_out_d, lvec_d, w_in, w_a, w_x, w_out, ntok=None):
    P = cx.P
    nt = (ntok // T) if ntok else NT
    XW = T + 3
    with contextlib.ExitStack() as st:
        lv, blv = load_vec(cx, st, lvec_d, "lvec", NLV * DC)

        def vec(i):
            return lv[:, i * DC:(i + 1) * DC]
        gain = vec(0)
        carry, bcar = load_vec(cx, st, carry_in_d, "carry")
        z = P.sbuf("z", [128, DC], F32, st); bz = Buf("z")
        y1 = P.sbuf("y1", [128, DC], F32, st); by1 = Buf("y1")
        pp = P.sbuf("pp", [128, DC], F32, st); bpp = Buf("pp")
        mk = P.sbuf("mk", [128, DC], F32, st); bmk = Buf("mk")
        nsp8 = P.sbuf("nsp8", [128, DC], F32, st); b8 = Buf("nsp8")
        nsp16 = P.sbuf("nsp16", [128, DC], F32, st); b16 = Buf("nsp16")
        P.op("act", lambda e: e.activation(z[:], vec(8), AF.Exp, scale=-1.0), reads=[blv], writes=[bz])
        P.op("act", lambda e: e.activation(y1[:], z[:], AF.Ln, bias=cx.onec[:], scale=1.0), reads=[bz, cx.b_onec], writes=[by1])
        P.op("dve", lambda e: e.tensor_scalar(pp[:], z[:], 0.2, -0.25, ALU.mult, ALU.add), reads=[bz], writes=[bpp])
        for cst in (1.0 / 3.0, -0.5, 1.0):
            P.op("dve", lambda e: e.tensor_tensor(pp[:], pp[:], z[:], ALU.mult), reads=[bpp, bz], writes=[bpp])
            P.op("dve", lambda e, cst=cst: e.tensor_scalar(pp[:], pp[:], cst, None, ALU.add), reads=[bpp], writes=[bpp])
        P.op("dve", lambda e: e.tensor_tensor(pp[:], pp[:], z[:], ALU.mult), reads=[bpp, bz], writes=[bpp])
        P.op("dve", lambda e: e.tensor_scalar(mk[:], z[:], 0.05, None, ALU.is_lt), reads=[bz], writes=[bmk])
        P.op("dve", lambda e: e.tensor_tensor(pp[:], pp[:], y1[:], ALU.subtract), reads=[bpp, by1], writes=[bpp])
        P.op("dve", lambda e: e.tensor_tensor(pp[:], pp[:], mk[:], ALU.mult), reads=[bpp, bmk], writes=[bpp])
        P.op("dve", lambda e: e.tensor_tensor(pp[:], pp[:], y1[:], ALU.add), reads=[bpp, by1], writes=[bpp])
        P.op("dve", lambda e: e.tensor_scalar(nsp8[:], pp[:], -8.0, None, ALU.mult), reads=[bpp], writes=[b8])
        P.op("dve", lambda e: e.tensor_scalar(nsp16[:], pp[:], -16.0, None, ALU.mult), reads=[bpp], writes=[b16])
        wa = P.sbuf("wa", [128, DC, 128], BF16, st); bwa = Buf("wa")
        wx = P.sbuf("wx", [128, DC, 128], BF16, st); bwx = Buf("wx")
        P.dma("pool", wa[:], w_a.rearrange("h i j -> i h j"), writes=[bwa])
        P.dma("pool", wx[:], w_x.rearrange("h i j -> i h j"), writes=[bwx])
        hb = P.sbuf("hb", [128, DC, T], F32, st); bh = Buf("hb")
        u = P.sbuf("u", [128, DC, T], BF16, st); bu = Buf("u")
        xrf = P.sbuf("xrf", [128, DC, XW], F32, st)
        bxr = [Buf("xrf%d" % c) for c in range(DC)]
        yb = P.sbuf("yb", [128, DC, T], BF16, st); byb = Buf("yb")
        hh = P.sbuf("hh", [128, DC, HALO], F32, st); bhh = Buf("hh")
        uh = P.sbuf("uh", [128, DC, HALO], BF16, st); buh = Buf("uh")
        NB = 2
        ggs = cx.scratch(st, "gg", [128, T], BF16, NB)
        xcs = cx.scratch(st, "xc", [128, T], F32, NB)
        xcbs = cx.scratch(st, "xcb", [128, T], BF16, NB)
        rs_ = cx.scratch(st, "r", [128, T], F32, NB)
        igs = cx.scratch(st, "ig", [128, T], F32, NB)
        as_ = cx.scratch(st, "a", [128, T], F32, NB)
        a2s = cx.scratch(st, "a2", [128, T], F32, NB)
        hls = cx.scratch(st, "hl", [128, T], F32, NB)
        bhin, bhout, bhalo, bco = Buf("hin"), Buf("hout"), Buf("halo"), Buf("cout")
        hin_v, hout_v = hview(hin), hview(hout)
        P.dma("sp", hh[:], hview(halo_d), reads=[bhalo], writes=[bhh])
        rmsnorm_tile(cx, st, hh, bhh, gain, blv, HALO, uh, buh)

        def evac_halo(c, ps, bb):
            P.op("dve", lambda e, c=c, ps=ps: e.tensor_copy(xrf[:, c, T:XW], ps[:, HALO - 3:HALO]),
                 reads=[bb], writes=[bxr[c]])
        jobs = [(w_in[:, D + c * 128:D + (c + 1) * 128], c) for c in range(DC)]
        stream_proj(cx, st, jobs, DC, lambda kc: uh[:, kc, :], [buh], HALO, evac_halo, nslab=3, tag="win")

        for t in range(nt):
            P.dma("sp", hb[:], hin_v[:, :, t * T:(t + 1) * T], reads=[bhin], writes=[bh])
            rmsnorm_tile(cx, st, hb, bh, gain, blv, T, u, bu)

            def evac_in(key, ps, bb):
                kind, c = key
                j = c % NB
                if kind == "g":
                    P.op("act", lambda e, ps=ps, j=j: e.activation(ggs[j][0][:], ps, AF.Gelu_apprx_tanh),
                         reads=[bb], writes=[ggs[j][1]])
                else:
                    P.op("pool", lambda e, c=c: e.tensor_copy(xrf[:, c, 0:3], xrf[:, c, T:XW]),
                         reads=[bxr[c]], writes=[bxr[c]])
                    P.op("dve", lambda e, c=c, ps=ps: e.tensor_copy(xrf[:, c, 3:XW], ps), reads=[bb], writes=[bxr[c]])

            def stage1(c):
                jobs = [(w_in[:, c * 128:(c + 1) * 128], ("g", c)),
                        (w_in[:, D + c * 128:D + (c + 1) * 128], ("x", c))]
                stream_proj(cx, st, jobs, DC, lambda kc: u[:, kc, :], [bu], T, evac_in, nslab=3, tag="win")
                j = c % NB
                xc, bxc = xcs[j]
                xcb, bxcb = xcbs[j]
                P.op("dve", lambda e: e.tensor_scalar(xc[:], xrf[:, c, 0:T], lv[:, DC + c:DC + c + 1],
                                                      lv[:, 5 * DC + c:5 * DC + c + 1], ALU.mult, ALU.add),
                     reads=[bxr[c], blv], writes=[bxc])
                for k in (1, 2, 3):
                    P.op("dve", lambda e, k=k: e.scalar_tensor_tensor(
                        xc[:], xrf[:, c, k:k + T], lv[:, (1 + k) * DC + c:(1 + k) * DC + c + 1], xc[:], ALU.mult, ALU.add),
                        reads=[bxr[c], blv, bxc], writes=[bxc])
                P.op("pool", lambda e: e.tensor_copy(xcb[:], xc[:]), reads=[bxc], writes=[bxcb])

            def stage2(c):
                j = c % NB
                xc, bxc = xcs[j]
                xcb, bxcb = xcbs[j]
                gg, bgg = ggs[j]
                r, br = rs_[j]
                ig, big = igs[j]
                a, ba = as_[j]
                a2, ba2 = a2s[j]
                hl, bhl = hls[j]
                bank_r, bbr = cx.mm_bank()
                P.op("pe", lambda e: e.matmul(bank_r[:, 0:T], wa[:, c, :], xcb[:], start=True, stop=True),
                     reads=[bwa, bxcb], writes=[bbr])
                bank_x, bbx = cx.mm_bank()
                P.op("pe", lambda e: e.matmul(bank_x[:, 0:T], wx[:, c, :], xcb[:], start=True, stop=True),
                     reads=[bwx, bxcb], writes=[bbx])
                P.op("act", lambda e: e.activation(r[:], bank_r[:, 0:T], AF.Sigmoid, bias=lv[:, 6 * DC + c:6 * DC + c + 1]),
                     reads=[bbr, blv], writes=[br])
                P.op("act", lambda e: e.activation(ig[:], bank_x[:, 0:T], AF.Sigmoid, bias=lv[:, 7 * DC + c:7 * DC + c + 1]),
                     reads=[bbx, blv], writes=[big])
                P.op("act", lambda e: e.activation(a[:], r[:], AF.Exp, scale=nsp8[:, c:c + 1]), reads=[br, b8], writes=[ba])
                P.op("act", lambda e: e.activation(a2[:], r[:], AF.Exp, scale=nsp16[:, c:c + 1]), reads=[br, b16], writes=[ba2])
                P.op("act", lambda e: e.activation(a2[:], a2[:], AF.Sqrt, bias=cx.onec[:], scale=-1.0),
                     reads=[ba2, cx.b_onec], writes=[ba2])
                P.op("dve", lambda e: e.tensor_tensor(ig[:], ig[:], a2[:], ALU.mult), reads=[big, ba2], writes=[big])
                P.op("dve", lambda e: e.tensor_tensor(ig[:], ig[:], xc[:], ALU.mult), reads=[big, bxc], writes=[big])
                P.op("dve", lambda e: e.tensor_tensor_scan(hl[:], a[:], ig[:], carry[:, c:c + 1], ALU.mult, ALU.add),
                     reads=[ba, big, bcar], writes=[bhl])
                P.op("pool", lambda e: e.tensor_copy(carry[:, c:c + 1], hl[:, T - 1:T]), reads=[bhl], writes=[bcar])
                P.op("dve", lambda e: e.tensor_tensor(yb[:, c, :], hl[:], gg[:], ALU.mult), reads=[bhl, bgg], writes=[byb])

            for c in range(DC + 1):
                if c < DC:
                    stage1(c)
                if c >= 1:
                    stage2(c - 1)

            def evac_o(oc, ps, bb):
                P.op("dve", lambda e, oc=oc, ps=ps: e.tensor_tensor(hb[:, oc, :], ps, hb[:, oc, :], ALU.add),
                     reads=[bb, bh], writes=[bh])
            jobs = [(w_out[:, oc * 128:(oc + 1) * 128], oc) for oc in range(DC)]
            stream_proj(cx, st, jobs, DC, lambda kc: yb[:, kc, :], [byb], T, evac_o, nslab=3, tag="wout")
            P.dma("sp", hout_v[:, :, t * T:(t + 1) * T], hb[:], reads=[bh], writes=[bhout])
        P.dma("sp", carry_out_d, carry[:], reads=[bcar], writes=[bco])
        cx.end_phase()


def phase_xattn(cx, hin, hout, memT_d, vecs_d, ident_d, wq, wk, wv, wo, ntok=None):
    P = cx.P
    nt = (ntok // T) if ntok else NT
    HD = D // NH
    with contextlib.ExitStack() as st:
        vv, bvv = load_vec(cx, st, vecs_d, "avec", 2 * DC)
        mgain, gain = vv[:, 0:DC], vv[:, DC:2 * DC]
        ident = P.sbuf("ident", [128, 128], BF16, st); bid = Buf("ident")
        P.dma("pool", ident[:], ident_d, writes=[bid])
        hb = P.sbuf("hb", [128, DC, T], F32, st); bh = Buf("hb")
        u = P.sbuf("u", [128, DC, T], BF16, st); bu = Buf("u")
        qT = P.sbuf("qT", [128, DC, T], BF16, st); bq = Buf("qT")
        oT = P.sbuf("oT", [128, DC, T], BF16, st); bo = Buf("oT")
        kT = P.sbuf("kT", [128, DC, ML], BF16, st); bk = Buf("kT")
        v = P.sbuf("v", [128, 2, D], BF16, st); bv = Buf("v")
        bhin, bhout, bmem = Buf("hin"), Buf("hout"), Buf("mem")
        hin_v, hout_v = hview(hin), hview(hout)
        with contextlib.ExitStack() as st2:
            mraw = P.sbuf("mraw", [128, DC, ML], F32, st2); bmr = Buf("mraw")
            mT = P.sbuf("mT", [128, DC, ML], BF16, st2); bmT = Buf("mT")
            P.dma("sp", mraw[:], hview(memT_d), reads=[bmem], writes=[bmr])
            rmsnorm_tile(cx, st2, mraw, bmr, mgain, bvv, ML, mT, bmT)

            def evac_k(jc, ps, bb):
                P.op("act", lambda e, jc=jc, ps=ps: e.activation(kT[:, jc, :], ps, AF.Copy), reads=[bb], writes=[bk])
            jobs = [(wk[:, jc * 128:(jc + 1) * 128], jc) for jc in range(DC)]
            stream_proj(cx, st2, jobs, DC, lambda kc: mT[:, kc, :], [bmT], ML, evac_k, nslab=3, tag="wqo")
            wvs = [(P.sbuf("wvs", [128, DC, 512], BF16, st2), Buf("wvs%d" % i)) for i in range(2)]
            for blk in range(4):
                ws, bws = wvs[blk % 2]
                P.dma("pool", ws[:], wv[:, blk * 512:(blk + 1) * 512].rearrange("(k p) n -> p k n", p=128), writes=[bws])
                for mc in range(2):
                    bank, bb = cx.mm_bank()
                    for kc in range(DC):
                        P.op("pe", lambda e, kc=kc, mc=mc, ws=ws, bank=bank: e.matmul(
                            bank[:], mT[:, kc, mc * 128:(mc + 1) * 128], ws[:, kc, :], start=(kc == 0), stop=(kc == DC - 1)),
                            reads=[bmT, bws], writes=[bb])
                    P.op("dve", lambda e, mc=mc, blk=blk, bank=bank: e.tensor_copy(v[:, mc, blk * 512:(blk + 1) * 512], bank[:]),
                         reads=[bb], writes=[bv])
            cx.end_phase()
        NB = 2
        pes = cx.scratch(st, "pe_", [128, ML], F32, NB)
        pns = cx.scratch(st, "pn", [128, ML], BF16, NB)
        mxs = cx.scratch(st, "mx", [128, 1], F32, NB)
        sss = cx.scratch(st, "ss", [128, 1], F32, NB)
        trs = cx.scratch(st, "tr", [128, 128], F32, NB)
        scss = cx.scratch(st, "scs", [128, ML], F32, NB)
        pTs = cx.scratch(st, "pT", [128, 2, T], BF16, 2)
        sc_bank, bsc = cx.banks[6], cx.bbuf[6]
        bscs = [Buf("sc0"), Buf("sc1")]
        bTs = [Buf("T0"), Buf("T1")]
        for t in range(nt):
            P.dma("sp", hb[:], hin_v[:, :, t * T:(t + 1) * T], reads=[bhin], writes=[bh])
            rmsnorm_tile(cx, st, hb, bh, gain, bvv, T, u, bu)

            def evac_q(jc, ps, bb):
                P.op("act", lambda e, jc=jc, ps=ps: e.activation(qT[:, jc, :], ps, AF.Copy, scale=float(HD) ** -0.5),
                     reads=[bb], writes=[bq])
            jobs = [(wq[:, jc * 128:(jc + 1) * 128], jc) for jc in range(DC)]
            stream_proj(cx, st, jobs, DC, lambda kc: u[:, kc, :], [bu], T, evac_q, nslab=3, tag="wqo")

            def s_scores(i):
                hd, sb = divmod(i, 4)
                j = i % NB
                bank, bb = cx.mm_bank()
                for j4 in range(4):
                    P.op("pe", lambda e, j4=j4: e.matmul(bank[:, 0:ML], qT[:, 4 * hd + j4, sb * 128:(sb + 1) * 128],
                                                         kT[:, 4 * hd + j4, :], start=(j4 == 0), stop=(j4 == 3)),
                         reads=[bq, bk], writes=[bb])
                pe_, bpe = pes[j]
                pn, bpn = pns[j]
                mx, bmx = mxs[j]
                ss, bss = sss[j]
                tr, btr = trs[j]
                scs, bscs_ = scss[j]
                P.op("act", lambda e: e.activation(scs[:], bank[:, 0:ML], AF.Copy), reads=[bb], writes=[bscs_])
                P.op("dve", lambda e: e.tensor_tensor(tr[:, 0:128], scs[:, 0:128], scs[:, 128:256], ALU.max), reads=[bscs_], writes=[btr])
                w = 64
                while w >= 1:
                    P.op("dve", lambda e, w=w: e.tensor_tensor(tr[:, 0:w], tr[:, 0:w], tr[:, w:2 * w], ALU.max), reads=[btr], writes=[btr])
                    w //= 2
                P.op("dve", lambda e: e.tensor_scalar(mx[:], tr[:, 0:1], -1.0, None, ALU.mult), reads=[btr], writes=[bmx])
                P.op("act", lambda e: e.activation(pe_[:], scs[:], AF.Exp, bias=mx[:], scale=1.0),
                     reads=[bscs_, bmx], writes=[bpe])
                P.op("dve", lambda e: e.tensor_tensor(tr[:, 0:128], pe_[:, 0:128], pe_[:, 128:256], ALU.add), reads=[bpe, btr], writes=[btr])
                w = 64
                while w >= 1:
                    P.op("dve", lambda e, w=w: e.tensor_tensor(tr[:, 0:w], tr[:, 0:w], tr[:, w:2 * w], ALU.add), reads=[btr], writes=[btr])
                    w //= 2
                P.op("dve", lambda e: e.reciprocal(ss[:], tr[:, 0:1]), reads=[btr], writes=[bss])
                P.op("dve", lambda e: e.tensor_scalar(pn[:], pe_[:], ss[:, 0:1], None, ALU.mult), reads=[bpe, bss], writes=[bpn])

            def s_pv(i):
                hd, sb = divmod(i, 4)
                j = i % NB
                pn, bpn = pns[j]
                pT, bpT = pTs[hd % 2]
                for mc in range(2):
                    off = ((i % 2) * 2 + mc) * 128
                    P.op("pe", lambda e, mc=mc, off=off: e.transpose(cx.bankT[:, off:off + 128], pn[:, mc * 128:(mc + 1) * 128], ident[:]),
                         reads=[bpn, bid], writes=[bTs[i % 2]])
                    P.op("act" if mc else "dve",
                         (lambda e, mc=mc, off=off: e.activation(pT[:, mc, sb * 128:(sb + 1) * 128], cx.bankT[:, off:off + 128], AF.Copy)) if mc else
                         (lambda e, mc=mc, off=off: e.tensor_copy(pT[:, mc, sb * 128:(sb + 1) * 128], cx.bankT[:, off:off + 128])),
                         reads=[bTs[i % 2]], writes=[bpT])
                if sb == 3:
                    for d4 in range(4):
                        dc = 4 * hd + d4
                        bank, bb = cx.mm_bank()
                        for mc in range(2):
                            P.op("pe", lambda e, mc=mc, dc=dc, bank=bank: e.matmul(
                                bank[:], v[:, mc, dc * 128:(dc + 1) * 128], pT[:, mc, :], start=(mc == 0), stop=(mc == 1)),
                                reads=[bv, bpT], writes=[bb])
                        P.op("act" if d4 % 2 else "dve",
                             (lambda e, dc=dc, bank=bank: e.activation(oT[:, dc, :], bank[:], AF.Copy)) if d4 % 2 else
                             (lambda e, dc=dc, bank=bank: e.tensor_copy(oT[:, dc, :], bank[:])),
                             reads=[bb], writes=[bo])

            if XDBG == 1:
                P.op("dve", lambda e: e.tensor_copy(oT[:], qT[:]), reads=[bq], writes=[bo])
            for i in range(17 if XDBG != 1 else 0):
                if i < 16:
                    s_scores(i)
                if i >= 1 and XDBG != 2:
                    s_pv(i - 1)
            if XDBG == 2:
                P.op("dve", lambda e: e.tensor_copy(oT[:], qT[:]), reads=[bq], writes=[bo])

            def evac_o(oc, ps, bb):
                P.op("dve", lambda e, oc=oc, ps=ps: e.tensor_tensor(hb[:, oc, :], ps, hb[:, oc, :], ALU.add),
                     reads=[bb, bh], writes=[bh])
            jobs = [(wo[:, oc * 128:(oc + 1) * 128], oc) for oc in range(DC)]
            stream_proj(cx, st, jobs, DC, lambda kc: oT[:, kc, :], [bo], T, evac_o, nslab=3, tag="wqo")
            P.dma("sp", hout_v[:, :, t * T:(t + 1) * T], hb[:], reads=[bh], writes=[bhout])
        cx.end_phase()


def _din(nc, name, shape):
    return nc.dram_tensor(name, list(shape), F32, kind="ExternalInput").ap()


def _dout(nc, name, shape):
    return nc.dram_tensor(name, list(shape), F32, kind="ExternalOutput").ap()


def _finish(cx, out_bufs_wait=True):
    P = cx.P
    P.barrier()
    P.emit()
    P.close()
    return cx.nc


def build_ffn():
    nc = bass.Bass("TRN2", target_bir_lowering=False)
    hin = _din(nc, "hin", [D, NTOK]); gain = _din(nc, "gain", [128, DC])
    wg = _din(nc, "wg", [D, FF]); wu = _din(nc, "wu", [D, FF]); wd = _din(nc, "wd", [FF, D])
    hout = _dout(nc, "hout", [D, NTOK])
    cx = Ctx(nc)
    phase_ffn(cx, hin, hout, gain, wg, wu, wd)
    return _finish(cx)


def build_final():
    nc = bass.Bass("TRN2", target_bir_lowering=False)
    hin = _din(nc, "hin", [D, NTOK]); gain = _din(nc, "gain", [128, DC])
    hout = _dout(nc, "hout", [D, NTOK])
    cx = Ctx(nc)
    phase_final(cx, hin, hout, gain)
    return _finish(cx)


def build_pool():
    nc = bass.Bass("TRN2", target_bir_lowering=False)
    hin = _din(nc, "hin", [D, NTOK]); halo = _din(nc, "halo", [D, HALO]); gain = _din(nc, "gain", [128, DC])
    pscale = _din(nc, "pscale", [128, DC]); invcnt = _din(nc, "invcnt", [128, 64]); pw = _din(nc, "pw", [4, 512, 512])
    hout = _dout(nc, "hout", [D, NTOK])
    cx = Ctx(nc)
    phase_pool(cx, hin, hout, halo, gain, pscale, invcnt, pw)
    return _finish(cx)


def vec_pm(v):
    return np.ascontiguousarray(np.asarray(v, np.float32).reshape(DC, 128).T)


def build_lru():
    nc = bass.Bass("TRN2", target_bir_lowering=False)
    hin = _din(nc, "hin", [D, NTOK]); halo = _din(nc, "halo", [D, HALO]); cin = _din(nc, "carry_in", [128, DC])
    lvec = _din(nc, "lvec", [128, NLV * DC]); w_in = _din(nc, "w_in", [D, 2 * D])
    w_a = _din(nc, "w_a", [DC, 128, 128]); w_x = _din(nc, "w_x", [DC, 128, 128]); w_out = _din(nc, "w_out", [D, D])
    hout = _dout(nc, "hout", [D, NTOK]); cout = _dout(nc, "carry_out", [128, DC])
    cx = Ctx(nc)
    phase_lru(cx, hin, hout, halo, cin, cout, lvec, w_in, w_a, w_x, w_out)
    return _finish(cx)


def build_xattn():
    nc = bass.Bass("TRN2", target_bir_lowering=False)
    hin = _din(nc, "hin", [D, NTOK]); memT = _din(nc, "memT", [D, ML]); avec = _din(nc, "avec", [128, 2 * DC])
    ident = _din(nc, "ident", [128, 128])
    wq = _din(nc, "wq", [D, D]); wk = _din(nc, "wk", [D, D]); wv = _din(nc, "wv", [D, D]); wo = _din(nc, "wo", [D, D])
    hout = _dout(nc, "hout", [D, NTOK])
    cx = Ctx(nc)
    phase_xattn(cx, hin, hout, memT, avec, ident, wq, wk, wv, wo)
    return _finish(cx)


_PROGS = {}


def _prog(name):
    if name not in _PROGS:
        _PROGS[name] = {"ffn": build_ffn, "pool": build_pool, "lru": build_lru, "xattn": build_xattn,
                        "final": build_final}[name]()
    return _PROGS[name]


def _run(name, maps):
    res = run_bass_kernel_spmd(_prog(name), maps, core_ids=list(range(NCORES)))
    return res.results


def kernel(**inp):
    f32 = np.float32
    x = np.asarray(inp["x"], f32)
    mem = np.asarray(inp["mem"], f32)
    g = {k: np.asarray(v, f32) for k, v in inp.items()}
    hT = [np.ascontiguousarray(x[c // 2, (c % 2) * NTOK:(c % 2 + 1) * NTOK, :].T) for c in range(NCORES)]
    memT = [np.ascontiguousarray(mem[c // 2].T) for c in range(NCORES)]
    ident = np.eye(128, dtype=f32)
    zero_halo = np.zeros((D, HALO), f32)
    zero_carry = np.zeros((128, DC), f32)
    invc = []
    for c in range(NCORES):
        a = np.zeros((128, 64), f32)
        for gi, w in enumerate((2, 4, 8, 16)):
            for t_ in range(16):
                a[:, gi * 16 + t_] = 1.0 / (min(t_ + 1, w) if c % 2 == 0 else w)
        invc.append(a)

    def halos():
        return [zero_halo if c % 2 == 0 else np.ascontiguousarray(hT[c - 1][:, NTOK - HALO:]) for c in range(NCORES)]

    def ffn(i, k):
        gain = vec_pm(g["ffn_norm"][i, k])
        maps = [{"hin": hT[c], "gain": gain, "wg": g["w_ffn_gate"][i, k], "wu": g["w_ffn_up"][i, k],
                 "wd": g["w_ffn_down"][i, k]} for c in range(NCORES)]
        r = _run("ffn", maps)
        return [r[c]["hout"] for c in range(NCORES)]

    for i in range(4):
        hT = ffn(i, 0)
        j = i // 2
        hl = halos()
        if i % 2 == 0:
            maps = [{"hin": hT[c], "halo": hl[c], "gain": vec_pm(g["mix_norm"][i]), "pscale": vec_pm(g["pool_scale"][j]),
                     "invcnt": invc[c], "pw": g["pool_w"][j]} for c in range(NCORES)]
            r = _run("pool", maps)
            hT = [r[c]["hout"] for c in range(NCORES)]
        else:
            lvec = np.ascontiguousarray(np.concatenate(
                [vec_pm(v) for v in [g["mix_norm"][i], g["lru_conv_w"][j, 0], g["lru_conv_w"][j, 1], g["lru_conv_w"][j, 2],
                                     g["lru_conv_w"][j, 3], g["lru_conv_b"][j], g["lru_b_a"][j], g["lru_b_x"][j],
                                     g["lru_lambda"][j]]], axis=1))
            carry = [zero_carry] * NCORES
            for rep in range(2):
                maps = [{"hin": hT[c], "halo": hl[c], "carry_in": carry[c], "lvec": lvec, "w_in": g["lru_w_in"][j],
                         "w_a": g["lru_w_a"][j], "w_x": g["lru_w_x"][j], "w_out": g["lru_w_out"][j]} for c in range(NCORES)]
                r = _run("lru", maps)
                carry = [zero_carry if c % 2 == 0 else np.ascontiguousarray(r[c - 1]["carry_out"]) for c in range(NCORES)]
            hT = [r[c]["hout"] for c in range(NCORES)]
        avec = np.ascontiguousarray(np.concatenate([vec_pm(g["mem_norm"]), vec_pm(g["xattn_norm"][i])], axis=1))
        maps = [{"hin": hT[c], "memT": memT[c], "avec": avec, "ident": ident, "wq": g["w_q"][i], "wk": g["w_k"][i],
                 "wv": g["w_v"][i], "wo": g["w_o"][i]} for c in range(NCORES)]
        r = _run("xattn", maps)
        hT = [r[c]["hout"] for c in range(NCORES)]
        hT = ffn(i, 1)
    maps = [{"hin": hT[c], "gain": vec_pm(g["final_norm"])} for c in range(NCORES)]
    r = _run("final", maps)
    out = np.empty((4, 2 * NTOK, D), f32)
    for c in range(NCORES):
        out[c // 2, (c % 2) * NTOK:(c % 2 + 1) * NTOK, :] = r[c]["hout"].T
    return out
```
